# Optimizing a Trainium2 kernel written in Bass

```python
import jax, jax.numpy as jnp
from jax import lax
import numpy as np

D_MODEL = 1024
BATCH = 4
SEQ = 4096
DEPTH = 2

CHUNK = 64
Q_BLOCK = 128
HEAD_DIM = 64
NORM_EPS = 1e-6

A_HEADS = 4
A_W = A_HEADS * HEAD_DIM
A_LEFT_CHUNKS = 8
A_MAX_REL = 128

B_HEADS = 4
B_Q_LORA = 256
B_KV_LORA = 128
B_NOPE = 64
B_ROPE = 32
B_V = 64
ROPE_THETA = 10000.0

C_HEADS = 4
C_W = C_HEADS * HEAD_DIM
FORGET_BIAS_MEAN = 3.0

CONV_CH = 256
CONV_K = 31

N_BRANCH = 4
BRANCH_W = 256
D_FF = 4 * D_MODEL

IN_SPLITS = (A_W, A_W, A_W,
             B_Q_LORA, B_KV_LORA, B_ROPE,
             C_W, C_W, C_W, C_HEADS,
             CONV_CH, CONV_CH)
IN_COLS = sum(IN_SPLITS)

kernel_name = 'hybrid_gated_streaming_encoder'


def _normal(key, shape, scale):
    return scale * jax.random.normal(key, shape, jnp.float32)


def rms_norm(x, g):
    xf = x.astype(jnp.float32)
    y = xf * lax.rsqrt(jnp.mean(xf * xf, axis=-1, keepdims=True) + NORM_EPS)
    return (y * g.astype(jnp.float32)).astype(x.dtype)


def layer_norm(x, g, b):
    xf = x.astype(jnp.float32)
    mu = jnp.mean(xf, axis=-1, keepdims=True)
    xc = xf - mu
    var = jnp.mean(xc * xc, axis=-1, keepdims=True)
    y = xc * lax.rsqrt(var + NORM_EPS) * g.astype(jnp.float32) + b.astype(jnp.float32)
    return y.astype(x.dtype)


def apply_rope(x, positions):
    half = x.shape[-1] // 2
    inv_freq = 1.0 / (ROPE_THETA ** (jnp.arange(half, dtype=jnp.float32) / half))
    ang = positions.astype(jnp.float32)[..., None] * inv_freq
    if x.ndim == 4:
        ang = ang[:, :, None, :]
    cos, sin = jnp.cos(ang), jnp.sin(ang)
    xf = x.astype(jnp.float32)
    x1, x2 = xf[..., :half], xf[..., half:]
    return jnp.concatenate([x1 * cos - x2 * sin, x2 * cos + x1 * sin], axis=-1).astype(x.dtype)


def chunk_band_attention(q, k, v, rel_table):
    b, s, h, d = q.shape
    nc = s // CHUNK
    w = A_LEFT_CHUNKS + 1
    qc = q.reshape(b, nc, CHUNK, h, d)

    def band(t):
        tc = t.reshape(b, nc, CHUNK, h, t.shape[-1])
        tp = jnp.pad(tc, ((0, 0), (A_LEFT_CHUNKS, 0), (0, 0), (0, 0), (0, 0)))
        return jnp.concatenate([tp[:, j:j + nc] for j in range(w)], axis=2)

    kb, vb = band(k), band(v)
    key_chunk = jnp.arange(nc)[:, None] + jnp.arange(w)[None, :] - A_LEFT_CHUNKS
    valid = jnp.repeat(key_chunk >= 0, CHUNK, axis=1)
    rel = (jnp.arange(CHUNK)[:, None] + A_LEFT_CHUNKS * CHUNK) - jnp.arange(w * CHUNK)[None, :]
    rel_idx = jnp.clip(rel, -A_MAX_REL, A_MAX_REL) + A_MAX_REL
    bias = rel_table[:, rel_idx].astype(jnp.float32)
    scores = jnp.einsum('bcqhd,bckhd->bchqk', qc, kb).astype(jnp.float32) * (d ** -0.5)
    scores = scores + bias[None, None]
    scores = jnp.where(valid[None, :, None, None, :], scores, -jnp.inf)
    p = jax.nn.softmax(scores, axis=-1).astype(v.dtype)
    out = jnp.einsum('bchqk,bckhd->bcqhd', p, vb)
    return out.reshape(b, s, h, -1)


def block_causal_attention(q, k, v, causal_unit, log_forget_cum=None):
    b, s, h, dk = q.shape
    nb = s // Q_BLOCK
    scale = dk ** -0.5
    key_unit = jnp.arange(s) // causal_unit
    qb = q.reshape(b, nb, Q_BLOCK, h, dk).swapaxes(0, 1)
    starts = jnp.arange(nb) * Q_BLOCK

    def attend(q_blk, start, f_blk):
        sc = jnp.einsum('bqhd,bkhd->bhqk', q_blk, k).astype(jnp.float32) * scale
        if f_blk is not None:
            f_k = log_forget_cum.transpose(0, 2, 1)[:, :, None, :]
            sc = sc + f_blk.transpose(0, 2, 1)[..., None] - f_k
        q_unit = (start + jnp.arange(Q_BLOCK)) // causal_unit
        mask = key_unit[None, :] <= q_unit[:, None]
        sc = jnp.where(mask[None, None], sc, -jnp.inf)
        p = jax.nn.softmax(sc, axis=-1).astype(v.dtype)
        return jnp.einsum('bhqk,bkhd->bqhd', p, v)

    if log_forget_cum is None:
        out = lax.map(lambda xs: attend(xs[0], xs[1], None), (qb, starts))
    else:
        fb = log_forget_cum.reshape(b, nb, Q_BLOCK, h).swapaxes(0, 1)
        out = lax.map(lambda xs: attend(xs[0], xs[1], xs[2]), (qb, starts, fb))
    return out.swapaxes(0, 1).reshape(b, s, h, -1)


def hybrid_layer(x, positions, g_mix, w_in, w_gate, b_gate, rel_bias, g_q_lat, w_uq, g_kv_lat, w_ukv,
                 b_forget, w_dw, b_dw, g_conv_ln, b_conv_ln, w_branch, w_o, g_ffn, w_up, w_down):
    b, s, _ = x.shape
    h = rms_norm(x, g_mix)
    split_points = [int(i) for i in np.cumsum(IN_SPLITS)[:-1]]
    (a_q, a_k, a_v, b_qlat, b_kvlat, b_krope,
     c_q, c_k, c_v, c_f, d_val, d_gate) = jnp.split(h @ w_in, split_points, axis=-1)

    def heads(t, n):
        return t.reshape(b, s, n, -1)

    out_a = chunk_band_attention(heads(a_q, A_HEADS), heads(a_k, A_HEADS), heads(a_v, A_HEADS), rel_bias)

    q = (rms_norm(b_qlat, g_q_lat) @ w_uq).reshape(b, s, B_HEADS, B_NOPE + B_ROPE)
    q = jnp.concatenate([q[..., :B_NOPE], apply_rope(q[..., B_NOPE:], positions)], axis=-1)
    kv = (rms_norm(b_kvlat, g_kv_lat) @ w_ukv).reshape(b, s, B_HEADS, B_NOPE + B_V)
    k_rot = apply_rope(b_krope, positions)
    k = jnp.concatenate([kv[..., :B_NOPE],
                         jnp.broadcast_to(k_rot[:, :, None, :], (b, s, B_HEADS, B_ROPE))], axis=-1)
    out_b = block_causal_attention(q, k, kv[..., B_NOPE:], CHUNK)

    log_f = jax.nn.log_sigmoid(c_f.astype(jnp.float32) + b_forget.astype(jnp.float32))
    f_cum = jnp.cumsum(log_f, axis=1)
    out_c = block_causal_attention(heads(c_q, C_HEADS), heads(c_k, C_HEADS), heads(c_v, C_HEADS), 1, f_cum)

    u = d_val * jax.nn.sigmoid(d_gate)
    u = lax.conv_general_dilated(u, w_dw[:, None, :], window_strides=(1,), padding=[(CONV_K - 1, 0)],
                                 dimension_numbers=('NWC', 'WIO', 'NWC'),
                                 feature_group_count=CONV_CH) + b_dw
    out_d = jax.nn.silu(layer_norm(u, g_conv_ln, b_conv_ln))

    branches = jnp.stack([out_a.reshape(b, s, BRANCH_W), out_b.reshape(b, s, BRANCH_W),
                          out_c.reshape(b, s, BRANCH_W), out_d], axis=2)
    proj = jnp.einsum('bsnc,ncd->bsnd', branches, w_branch)
    gates = jax.nn.sigmoid(h @ w_gate + b_gate).reshape(b, s, N_BRANCH, D_MODEL)
    x = x + jnp.sum(gates * proj, axis=2) @ w_o

    hf = rms_norm(x, g_ffn)
    x = x + jnp.square(jax.nn.relu(hf @ w_up)) @ w_down
    return x


def setup_inputs(seed: int = 0) -> dict:
    key = jax.random.key(seed)
    ks = jax.random.split(key, 24)
    L = DEPTH
    x = _normal(ks[0], (BATCH, SEQ, D_MODEL), 1.0)
    offset = jax.random.randint(ks[1], (BATCH, 1), 0, 4 * SEQ, dtype=jnp.int32)
    positions = offset + jnp.arange(SEQ, dtype=jnp.int32)[None, :]
    return dict(
        x=x,
        positions=positions,
        g_mix=1.0 + _normal(ks[2], (L, D_MODEL), 0.05),
        w_in=_normal(ks[3], (L, D_MODEL, IN_COLS), D_MODEL ** -0.5),
        w_gate=_normal(ks[4], (L, D_MODEL, N_BRANCH * D_MODEL), D_MODEL ** -0.5),
        b_gate=_normal(ks[5], (L, N_BRANCH * D_MODEL), 0.02),
        rel_bias=_normal(ks[6], (L, A_HEADS, 2 * A_MAX_REL + 1), 0.2),
        g_q_lat=1.0 + _normal(ks[7], (L, B_Q_LORA), 0.05),
        w_uq=_normal(ks[8], (L, B_Q_LORA, B_HEADS * (B_NOPE + B_ROPE)), B_Q_LORA ** -0.5),
        g_kv_lat=1.0 + _normal(ks[9], (L, B_KV_LORA), 0.05),
        w_ukv=_normal(ks[10], (L, B_KV_LORA, B_HEADS * (B_NOPE + B_V)), B_KV_LORA ** -0.5),
        b_forget=FORGET_BIAS_MEAN + _normal(ks[11], (L, C_HEADS), 0.5),
        w_dw=_normal(ks[12], (L, CONV_K, CONV_CH), CONV_K ** -0.5),
        b_dw=_normal(ks[13], (L, CONV_CH), 0.02),
        g_conv_ln=1.0 + _normal(ks[14], (L, CONV_CH), 0.05),
        b_conv_ln=_normal(ks[15], (L, CONV_CH), 0.02),
        w_branch=_normal(ks[16], (L, N_BRANCH, BRANCH_W, D_MODEL), BRANCH_W ** -0.5),
        w_o=_normal(ks[17], (L, D_MODEL, D_MODEL), D_MODEL ** -0.5),
        g_ffn=1.0 + _normal(ks[18], (L, D_MODEL), 0.05),
        w_up=_normal(ks[19], (L, D_MODEL, D_FF), D_MODEL ** -0.5),
        w_down=_normal(ks[20], (L, D_FF, D_MODEL), D_FF ** -0.5),
        g_final=1.0 + _normal(ks[21], (D_MODEL,), 0.05),
    )


def reference(x, positions, g_mix, w_in, w_gate, b_gate, rel_bias, g_q_lat, w_uq, g_kv_lat, w_ukv,
              b_forget, w_dw, b_dw, g_conv_ln, b_conv_ln, w_branch, w_o, g_ffn, w_up, w_down, g_final):
    for l in range(DEPTH):
        x = hybrid_layer(x, positions, g_mix[l], w_in[l], w_gate[l], b_gate[l], rel_bias[l],
                         g_q_lat[l], w_uq[l], g_kv_lat[l], w_ukv[l], b_forget[l], w_dw[l], b_dw[l],
                         g_conv_ln[l], b_conv_ln[l], w_branch[l], w_o[l], g_ffn[l], w_up[l], w_down[l])
    return rms_norm(x, g_final)
```

```python
import contextlib
import numpy as np
import ml_dtypes
import concourse.bass as bass
import concourse.mybir as mybir
from concourse.bass_utils import run_bass_kernel_spmd

F32 = mybir.dt.float32
BF16 = mybir.dt.bfloat16
I32 = mybir.dt.int32
AF = mybir.ActivationFunctionType
ALU = mybir.AluOpType

L = 2
D = 1024
T = 2048
NB = 4
EPS = 1e-6
NEG = -30000.0

KTA, KTB, KTC, VA, VB, VC, UT, XR = 0, 256, 640, 932, 1188, 1444, 1700, 1704
QTA, QTB, QTC, LR = 0, 256, 640, 920
NVEC = 120
WI_COLS = 3072

ENGS = ["tensor", "vector", "scalar", "gpsimd", "sync"]
DEBUG = ""


class Op:
    __slots__ = ("eng", "fn", "dma", "deps", "need_inc", "cnt", "semkey")

    def __init__(self, eng, fn, dma):
        self.eng = eng
        self.fn = fn
        self.dma = dma
        self.deps = set()
        self.need_inc = False
        self.cnt = None
        self.semkey = None


class Sched:
    def __init__(self, nc, n_dma_sems=16):
        self.nc = nc
        self.ops = []
        self.last_w = {}
        self.readers = {}
        self.n_dma_sems = n_dma_sems
        self.last_on = {}
        self.dma_since_bar = []

    def op(self, eng, fn, reads=(), writes=(), dma=False):
        o = Op(eng, fn, dma)
        deps = o.deps
        for k in reads:
            w = self.last_w.get(k)
            if w is not None:
                deps.add(w)
            if isinstance(k, tuple) and k[0] == "ps":
                rd = self.readers.get(k)
                if rd:
                    for e2, r in rd[0].items():
                        if e2 != eng:
                            deps.add(r)
        for k in writes:
            w = self.last_w.get(k)
            if w is not None:
                deps.add(w)
            rd = self.readers.get(k)
            if rd:
                for r in rd[0].values():
                    deps.add(r)
                for r in rd[1]:
                    deps.add(r)
        deps.discard(o)
        for k in reads:
            rd = self.readers.get(k)
            if rd is None:
                rd = self.readers[k] = ({}, [])
            if dma:
                rd[1].append(o)
            else:
                rd[0][eng] = o
        for k in writes:
            self.last_w[k] = o
            self.readers[k] = ({}, [])
        self.ops.append(o)
        if dma:
            self.dma_since_bar.append(o)
        else:
            self.last_on[eng] = o
        return o

    def barrier(self):
        lasts = list(self.last_on.values()) + list(self.dma_since_bar)
        self.dma_since_bar = []
        for e in ENGS:
            o = Op(e, None, False)
            o.deps = set(lasts)
            self.ops.append(o)

    def emit(self):
        nc = self.nc
        per = {e: [] for e in ENGS}
        for o in self.ops:
            per[o.eng].append(o)
        for o in self.ops:
            nd = []
            for d in o.deps:
                if d.eng == "tensor" and o.eng == "tensor" and not d.dma and not o.dma:
                    continue
                nd.append(d)
                d.need_inc = True
            o.deps = nd
        cnt = {e: 0 for e in ENGS}
        dcnt = {}
        drr = {e: 0 for e in ENGS}
        for e in ENGS:
            for o in per[e]:
                if o.dma:
                    k = (e, drr[e] % self.n_dma_sems)
                    drr[e] += 1
                    dcnt[k] = dcnt.get(k, 0) + 16
                    o.semkey = k
                    o.cnt = dcnt[k]
                elif o.need_inc and o.fn is not None:
                    cnt[e] += 1
                    o.semkey = e
                    o.cnt = cnt[e]
        self.stats = {e: (len(per[e]), cnt[e]) for e in ENGS}
        semkeys = list(ENGS) + sorted(dcnt.keys())
        with contextlib.ExitStack() as st:
            sems = {}
            for k in semkeys:
                nm = k if isinstance(k, str) else f"d_{k[0]}_{k[1]}"
                sems[k] = st.enter_context(nc.semaphore("s_" + nm))
            block = st.enter_context(nc.Block())

            def mk(e):
                def body(engobj):
                    known = {}
                    for o in per[e]:
                        need = {}
                        for d in o.deps:
                            if d.cnt is None:
                                continue
                            if need.get(d.semkey, 0) < d.cnt:
                                need[d.semkey] = d.cnt
                        for k, v in need.items():
                            if known.get(k, 0) >= v:
                                continue
                            engobj.wait_ge(sems[k], v)
                            known[k] = v
                        if o.fn is None:
                            continue
                        if o.dma and o.cnt > 16 and known.get(o.semkey, 0) < o.cnt - 16:
                            engobj.wait_ge(sems[o.semkey], o.cnt - 16)
                            known[o.semkey] = o.cnt - 16
                        ins = o.fn(engobj)
                        if o.dma:
                            ins.then_inc(sems[o.semkey], 16)
                        elif o.need_inc:
                            ins.then_inc(sems[o.semkey], 1)
                return body

            for e in ENGS:
                if per[e]:
                    getattr(block, e)(mk(e))


def build(phases, fused):
    nc = bass.Bass("TRN2", target_bir_lowering=False)
    S = Sched(nc)

    def din(name, shape, dt):
        return nc.dram_tensor(name, list(shape), dt, kind="ExternalInput").ap()

    def dout(name, shape, dt):
        return nc.dram_tensor(name, list(shape), dt, kind="ExternalOutput").ap()

    def dint(name, shape, dt):
        return nc.dram_tensor(name, list(shape), dt, kind="Internal").ap()

    layers = sorted({int(p.split("_")[1]) for p in phases if "_" in p})
    nl = len(layers)
    lidx = {l: i for i, l in enumerate(layers)}
    xT_in = din("xT_in", [D, T], F32)
    pos_in = din("pos_in", [32, T], I32)
    cst_in = din("cst_in", [128, 8], F32)
    vec_in = din("vec_in", [nl, 128, NVEC], F32)
    gfin_in = din("gfin_in", [128, 8], F32)
    idn_in = din("idn_in", [128, 128], F32)
    mbc_in = din("mbc_in", [8, 128, 512], F32)
    mA_in = din("mA_in", [8, 128, 512], F32)
    need_p1 = any(p.startswith("P1") for p in phases)
    need_p23 = any(p.startswith("P23") for p in phases)
    W = {}
    names_in = ["xT_in", "pos_in", "cst_in", "vec_in", "gfin_in", "idn_in", "mbc_in", "mA_in"]
    relT_in = None
    if need_p1:
        W["wi"] = din("wi", [nl, D, WI_COLS], F32)
        W["wuq"] = din("wuq", [nl, 256, 768], F32)
        W["wukv"] = din("wukv", [nl, 128, 512], F32)
        names_in += ["wi", "wuq", "wukv"]
    if need_p23:
        W["wg"] = din("wg", [nl, D, 4096], F32)
        W["wbr"] = din("wbr", [nl, 4, 256, D], F32)
        W["wo"] = din("wo", [nl, D, D], F32)
        W["wup"] = din("wup", [nl, D, 4096], F32)
        W["wdn"] = din("wdn", [nl, 4096, D], F32)
        relT_in = din("relT", [nl, 4, 8, 128, 512], F32)
        names_in += ["wg", "wbr", "wo", "wup", "wdn", "relT"]

    bufs = {}
    if fused:
        raise NotImplementedError
    else:
        first, last = phases[0], phases[-1]
        if first.startswith("P23"):
            bufs["xch_r"] = din("xch_i", [XR, T], BF16)
            bufs["xprev_r"] = din("xprev_i", [XR, T], BF16)
            bufs["loc_r"] = din("loc_i", [LR, T], BF16)
            bufs["uloc_r"] = din("uloc_i", [256, T], F32)
            names_in += ["xch_i", "xprev_i", "loc_i", "uloc_i"]
        if last.startswith("P1"):
            bufs["xch_w"] = dout("xch_o", [XR, T], BF16)
            bufs["loc_w"] = dout("loc_o", [LR, T], BF16)
            bufs["uloc_w"] = dout("uloc_o", [256, T], F32)
        xT_out = dout("xT_out", [D, T], F32)
        if "FIN" in phases:
            yT_out = dout("yT_out", [D, T], F32)

    with contextlib.ExitStack() as top:
        uid = [0]

        def sbt(st, name, shape, dt):
            uid[0] += 1
            return st.enter_context(nc.sbuf_tensor(f"sb{uid[0]}_{name}", list(shape), dt))

        xT = sbt(top, "xT", [128, 8, T], F32)
        cst = sbt(top, "cst", [128, 8], F32)
        vec = sbt(top, "vec", [128, nl, NVEC], F32)
        gfin = sbt(top, "gfin", [128, 8], F32)
        idn = sbt(top, "idn", [128, 128], BF16)
        on1024 = sbt(top, "on1024", [128, 128], BF16)
        on256 = sbt(top, "on256", [128, 128], BF16)
        on128 = sbt(top, "on128", [128, 128], BF16)
        ones_bf = sbt(top, "ones_bf", [128, 512], BF16)
        ones_f = sbt(top, "ones_f", [128, 512], F32)
        psum = [top.enter_context(nc.psum_tensor(f"ps{i}", [128, 512], F32)) for i in range(8)]
        pctr = [0]

        def PS(lo=0, hi=8):
            i = lo + (pctr[0] % (hi - lo))
            pctr[0] += 1
            return psum[i], ("ps", i)

        def DMA(q, out, in_, r, w):
            S.op(q, lambda e: e.dma_start(out=out, in_=in_), r, w, dma=True)

        def MM(out, lhsT, rhs, start, stop, r, w):
            S.op("tensor", lambda e: e.matmul(out, lhsT=lhsT, rhs=rhs, start=start, stop=stop), r, w)

        def ACT(out, in_, func, r, w, scale=1.0, bias=0.0):
            S.op("scalar", lambda e: e.activation(out=out, in_=in_, func=func, bias=bias, scale=scale), r, w)

        def TT(eng, out, in0, in1, op, r, w):
            S.op(eng, lambda e: e.tensor_tensor(out=out, in0=in0, in1=in1, op=op), r, w)

        def TS(eng, out, in0, s1, s2, op0, op1, r, w):
            if op1 is None:
                S.op(eng, lambda e: e.tensor_scalar(out=out, in0=in0, scalar1=s1, scalar2=None, op0=op0), r, w)
            else:
                S.op(eng, lambda e: e.tensor_scalar(out=out, in0=in0, scalar1=s1, scalar2=s2, op0=op0, op1=op1), r, w)

        def STT(out, in0, scalar, in1, op0, op1, r, w):
            S.op("vector", lambda e: e.scalar_tensor_tensor(out=out, in0=in0, scalar=scalar, in1=in1, op0=op0, op1=op1), r, w)

        def CP(eng, out, in_, r, w):
            if eng == "scalar":
                ACT(out, in_, AF.Copy, r, w)
            else:
                S.op(eng, lambda e: e.tensor_copy(out=out, in_=in_), r, w)

        def MSET(eng, ap, val, w):
            S.op(eng, lambda e: e.memset(ap, val), (), w)

        XK = [("xT", c, tb) for c in range(8) for tb in range(NB)]
        DMA("sync", xT[:], xT_in.rearrange("(c p) t -> p c t", p=128), (), XK)
        DMA("sync", cst[:], cst_in, (), ["cst"])
        DMA("sync", vec[:], vec_in.rearrange("l p n -> p l n"), (), ["vec"])
        DMA("sync", gfin[:], gfin_in, (), ["gfin"])
        DMA("gpsimd", idn[:], idn_in, (), ["idn"])
        MSET("vector", on1024[:], 1.0 / 1024, ["on1024"])
        MSET("vector", on256[:], 1.0 / 256, ["on256"])
        MSET("vector", on128[:], 1.0 / 128, ["on128"])
        MSET("vector", ones_bf[:], 1.0, ["ones_bf"])
        MSET("vector", ones_f[:], 1.0, ["ones_f"])

        def rstd_from(ps_ap, out_ap, tmp_ap, r, wk, tk):
            ACT(tmp_ap, ps_ap, AF.Ln, r, [tk], bias=EPS)
            ACT(out_ap, tmp_ap, AF.Exp, [tk], [wk], scale=-0.5)

        def rmsnorm_to(hdst, hkey, gcol0, li, st):
            sq = [sbt(st, f"nsq{i}_{hkey}", [128, 512], BF16) for i in range(3)]
            rs = [sbt(st, f"nrs{i}_{hkey}", [128, 512], F32) for i in range(2)]
            tl = [sbt(st, f"ntl{i}_{hkey}", [128, 512], F32) for i in range(2)]
            k = 0
            for tb in range(NB):
                cs = slice(tb * 512, (tb + 1) * 512)
                pt, pk = PS()
                for c in range(8):
                    s_ = sq[k % 3]
                    sk = ("nsq", hkey, k % 3)
                    k += 1
                    ACT(s_[:], xT[:, c, cs], AF.Square, [("xT", c, tb)], [sk])
                    MM(pt[:], on1024[:], s_[:], c == 0, c == 7, [sk, "on1024"], [pk])
                r_ = rs[tb % 2]
                rk = ("nrs", hkey, tb % 2)
                rstd_from(pt[:], r_[:], tl[tb % 2][:], [pk], rk, ("ntl", hkey, tb % 2))
                for c in range(8):
                    STT(hdst[:, c, cs], xT[:, c, cs], vec[:, li, gcol0 + c:gcol0 + c + 1], r_[:],
                        ALU.mult, ALU.mult, [("xT", c, tb), "vec", rk], [(hkey, c, tb)])

        def phase1(l):
            li = lidx[l]
            if DEBUG == "none":
                return
            xch_w, loc_w, uloc_w = bufs["xch_w"], bufs["loc_w"], bufs["uloc_w"]
            S.barrier()
            with contextlib.ExitStack() as st:
                h = sbt(st, "h", [128, 8, T], BF16)
                wbuf = [sbt(st, f"wb{i}", [128, 8, 512], BF16) for i in range(2)]
                wuq = sbt(st, "wuq", [128, 2, 768], BF16)
                wukv = sbt(st, "wukv", [128, 512], BF16)
                cosF = sbt(st, "cosF", [128, T], F32)
                sinS = sbt(st, "sinS", [128, T], F32)
                P8 = sbt(st, "P8", [128, T], F32)
                stg = [sbt(st, f"stg{i}", [128, 512], BF16) for i in range(8)]
                tmp = [sbt(st, f"tmp{i}", [128, 512], F32) for i in range(8)]
                qlg = sbt(st, "qlg", [128, 2, 512], BF16)
                kvlg = sbt(st, "kvlg", [128, 512], BF16)
                sqb = [sbt(st, f"sqb{i}", [128, 512], BF16) for i in range(3)]
                posi = sbt(st, "posi", [128, 512], I32)
                ki = sbt(st, "ki", [128, 512], I32)
                fsp = sbt(st, "fsp", [128, 512], F32)
                fparts = sbt(st, "fparts", [128, 9, 512], BF16)
                rk_t = sbt(st, "rk_t", [128, 16], F32)
                rqb = sbt(st, "rqb", [128, 512], F32)
                rkbb = sbt(st, "rkbb", [128, 512], F32)
                sctr = [0]
                tctr = [0]

                def STG():
                    i = sctr[0] % 8
                    sctr[0] += 1
                    return stg[i], ("stg", i)

                def TMP():
                    i = tctr[0] % 8
                    tctr[0] += 1
                    return tmp[i], ("tmp", i)

                DMA("gpsimd", wuq[:], W["wuq"][li].rearrange("(c p) n -> p c n", p=128), (), ["wuq"])
                DMA("gpsimd", wukv[:], W["wukv"][li], (), ["wukv"])
                R = slice(64, 96)
                for tb in range(NB):
                    cs = slice(tb * 512, (tb + 1) * 512)
                    DMA("sync", posi[R, :], pos_in[:, cs], (), ["posi"])
                    a_, ak = TMP()
                    CP("vector", a_[R, :], posi[R, :], ["posi"], [ak])
                    TS("vector", a_[R, :], a_[R, :], cst[R, 0:1], None, ALU.mult, None, [ak, "cst"], [ak])
                    for which in range(2):
                        b_, bk = TMP()
                        kf, kk = TMP()
                        if which == 0:
                            TS("vector", b_[R, :], a_[R, :], float(np.pi / 2), None, ALU.add, None, [ak], [bk])
                            src, srck = b_, bk
                        else:
                            src, srck = a_, ak
                        TS("vector", kf[R, :], src[R, :], float(1.0 / (2 * np.pi)), None, ALU.mult, None, [srck], [kk])
                        CP("vector", ki[R, :], kf[R, :], [kk], ["ki"])
                        CP("vector", kf[R, :], ki[R, :], ["ki"], [kk])
                        r1, r1k = TMP()
                        STT(r1[R, :], kf[R, :], -6.28125, src[R, :], ALU.mult, ALU.add, [kk, srck], [r1k])
                        STT(r1[R, :], kf[R, :], -0.0019353071795864769, r1[R, :], ALU.mult, ALU.add, [kk, r1k], [r1k])
                        TS("vector", r1[R, :], r1[R, :], 3.1415925, -3.1415925, ALU.min, ALU.max, [r1k], [r1k])
                        if which == 0:
                            ACT(cosF[R, cs], r1[R, :], AF.Sin, [r1k], [("cosF", tb)])
                        else:
                            ACT(sinS[R, cs], r1[R, :], AF.Sin, [r1k, "cst"], [("sinS", tb)], scale=cst[R, 1:2])
                if DEBUG == "rope":
                    S.barrier()
                    return
                with contextlib.ExitStack() as st2:
                    rmsnorm_to(h, "h", 0, li, st2)
                hkeys = lambda tb: [("h", c, tb) for c in range(8)]
                nb_col = vec[0:4, li, 57:58]
                negb = sbt(st, "negb", [128, 1], F32)
                TS("vector", negb[0:4, :], nb_col, -1.0, None, ALU.mult, None, ["vec"], ["negb"])

                wv = W["wi"][li].rearrange("(kc p) n -> p kc n", p=128)
                ngroups = 6
                gcols = [512, 512, 512, 512, 256, 512]

                def load_group(g):
                    wb = wbuf[g % 2]
                    DMA("gpsimd", wb[:, :, 0:gcols[g]], wv[:, :, g * 512:g * 512 + gcols[g]], (), [("wb", g % 2)])

                load_group(0)
                live = {}
                p8prev = [None]
                for g in range(ngroups):
                    if DEBUG.startswith("g") and g >= int(DEBUG[1:2]):
                        S.barrier()
                        return
                    if g + 1 < ngroups:
                        load_group(g + 1)
                    wb = wbuf[g % 2]
                    wk = ("wb", g % 2)
                    if g == 5:
                        for tt in range(16):
                            ts_ = slice(tt * 128, (tt + 1) * 128)
                            pt, pk = PS()
                            for c in range(8):
                                MM(pt[:], h[:, c, ts_], wb[:, c, :], c == 0, c == 7, [("h", c, tt // 4), wk], [pk])
                            s1, s1k = STG()
                            CP("scalar" if tt % 2 else "vector", s1[:], pt[:], [pk], [s1k])
                            va = xch_w[VA:VA + 256, :].rearrange("r (a c) -> (r a) c", c=256)
                            vc = xch_w[VC:VC + 256, :].rearrange("r (a c) -> (r a) c", c=256)
                            DMA("sync", va[ts_, :], s1[:, 0:256], [s1k], [("xch_w", "va", tt)])
                            DMA("sync", vc[ts_, :], s1[:, 256:512], [s1k], [("xch_w", "vc", tt)])
                        continue
                    names = [["Aq0", "Aq1", "Ak0", "Ak1"], ["ql0", "ql1", "kvl", "f"], ["kr", "krs", "Cq0", "Cq1"],
                             ["Ck0", "Ck1", "val0", "gate0"], ["val1", "gate1"]][g]
                    for tb in range(NB):
                        cs = slice(tb * 512, (tb + 1) * 512)
                        for ti, nm in enumerate(names):
                            pt, pk = PS()
                            for c in range(8):
                                MM(pt[:], wb[:, c, ti * 128:(ti + 1) * 128], h[:, c, cs], c == 0, c == 7,
                                   [wk, ("h", c, tb)], [pk])
                            ev = "scalar" if (ti + tb) % 2 else "vector"
                            if ":" in DEBUG and nm[:2] in DEBUG.split(":")[1].split(","):
                                continue
                            if nm in ("Aq0", "Aq1", "Ak0", "Ak1"):
                                s1, s1k = STG()
                                CP(ev, s1[:], pt[:], [pk], [s1k])
                                j = int(nm[2])
                                if nm[1] == "q":
                                    DMA("sync", loc_w[QTA + j * 128:QTA + (j + 1) * 128, cs], s1[:], [s1k], [("loc_w", nm, tb)])
                                else:
                                    DMA("sync", xch_w[KTA + j * 128:KTA + (j + 1) * 128, cs], s1[:], [s1k], [("xch_w", nm, tb)])
                            elif nm in ("ql0", "ql1"):
                                j = int(nm[2])
                                sq_ = sqb[j]
                                ACT(sq_[:], pt[:], AF.Square, [pk], [("sqb", j)])
                                TS("vector", qlg[:, j, :], pt[:], vec[:, li, 16 + j:17 + j], None, ALU.mult, None,
                                   [pk, "vec"], [("qlg", j)])
                                if j == 1:
                                    p2, p2k = PS()
                                    MM(p2[:], on256[:], sqb[0][:], True, False, [("sqb", 0), "on256"], [p2k])
                                    MM(p2[:], on256[:], sqb[1][:], False, True, [("sqb", 1)], [p2k])
                                    rq, rqk = rqb, "rqb"
                                    t_, tk = TMP()
                                    rstd_from(p2[:], rq[:], t_[:], [p2k], rqk, tk)
                                    for hh in range(4):
                                        pq, pqk = PS()
                                        pqs, pqsk = PS()
                                        for c in range(2):
                                            MM(pq[0:96, :], wuq[:, c, hh * 96:(hh + 1) * 96], qlg[:, c, :], c == 0, c == 1,
                                               ["wuq", ("qlg", c)], [pqk])
                                        for c in range(2):
                                            MM(pqs[0:96, :], wuq[:, c, 384 + hh * 96:384 + (hh + 1) * 96], qlg[:, c, :], c == 0, c == 1,
                                               ["wuq", ("qlg", c)], [pqsk])
                                        s1, s1k = STG()
                                        TT("vector", s1[0:64, :], pq[0:64, :], rq[0:64, :], ALU.mult, [pqk, rqk], [s1k])
                                        a1, a1k = TMP()
                                        a2, a2k = TMP()
                                        TT("vector", a1[R, :], pq[R, :], rq[R, :], ALU.mult, [pqk, rqk], [a1k])
                                        TT("vector", a2[R, :], pqs[R, :], rq[R, :], ALU.mult, [pqsk, rqk], [a2k])
                                        TT("gpsimd", a1[R, :], a1[R, :], cosF[R, cs], ALU.mult, [a1k, ("cosF", tb)], [a1k])
                                        TT("gpsimd", a2[R, :], a2[R, :], sinS[R, cs], ALU.mult, [a2k, ("sinS", tb)], [a2k])
                                        TT("gpsimd", s1[R, :], a1[R, :], a2[R, :], ALU.add, [a1k, a2k], [s1k])
                                        DMA("sync", loc_w[QTB + hh * 96:QTB + (hh + 1) * 96, cs], s1[0:96, :], [s1k],
                                            [("loc_w", "qb", hh, tb)])
                            elif nm == "kvl":
                                if "kvA" not in DEBUG:
                                    ACT(sqb[2][:], pt[:], AF.Square, [pk], [("sqb", 2)])
                                if "kvs0" in DEBUG:
                                    continue
                                if "kvB" not in DEBUG:
                                    TS("vector", kvlg[:], pt[:], vec[:, li, 18:19], None, ALU.mult, None, [pk, "vec"], ["kvlg"])
                                if "kvs1" in DEBUG:
                                    continue
                                p2, p2k = PS()
                                MM(p2[:], on128[:], sqb[2][:], True, True, [("sqb", 2), "on128"], [p2k])
                                if "kvs2" in DEBUG:
                                    continue
                                rkb, rkbk = rkbb, "rkbb"
                                t_, tk = TMP()
                                rstd_from(p2[:], rkb[:], t_[:], [p2k], rkbk, tk)
                                if "kvstop1" in DEBUG:
                                    continue
                                p3, p3k = PS()
                                for q4 in range(4):
                                    MM(p3[:, q4:q4 + 1], sqb[2][:, q4 * 128:(q4 + 1) * 128], on128[:, 0:1], True, True,
                                       [("sqb", 2), "on128"], [p3k])
                                t2, t2k = TMP()
                                rstd_from(p3[:, 0:4], rk_t[:, tb * 4:tb * 4 + 4], t2[:, 0:4], [p3k], ("rk_t", tb), t2k)
                                if "kvstop2" in DEBUG:
                                    continue
                                for hh in range(4):
                                    pq, pqk = PS()
                                    MM(pq[0:64, :], wukv[:, hh * 64:(hh + 1) * 64], kvlg[:], True, True, ["wukv", "kvlg"], [pqk])
                                    s1, s1k = STG()
                                    TT("vector", s1[0:64, :], pq[0:64, :], rkb[0:64, :], ALU.mult, [pqk, rkbk], [s1k])
                                    DMA("sync", xch_w[KTB + hh * 96:KTB + hh * 96 + 64, cs], s1[0:64, :], [s1k],
                                        [("xch_w", "kbn", hh, tb)])
                                if "kvstop3" in DEBUG:
                                    continue
                                vb = xch_w[VB:VB + 256, :].rearrange("r (a c) -> (r a) c", c=256)
                                for q4 in range(4):
                                    tt = tb * 4 + q4
                                    pq, pqk = PS()
                                    MM(pq[:, 0:256], kvlg[:, q4 * 128:(q4 + 1) * 128], wukv[:, 256:512], True, True,
                                       ["kvlg", "wukv"], [pqk])
                                    s1, s1k = STG()
                                    TS("vector", s1[:, 0:256], pq[:, 0:256], rk_t[:, tt:tt + 1], None, ALU.mult, None,
                                       [pqk, ("rk_t", tb)], [s1k])
                                    DMA("sync", vb[tt * 128:(tt + 1) * 128, :], s1[:, 0:256], [s1k], [("xch_w", "vb", tt)])
                            elif nm == "f":
                                F4 = slice(0, 4)
                                e_, ek = TMP()
                                ACT(e_[F4, :], pt[F4, :], AF.Exp, [pk, "negb"], [ek], scale=-1.0, bias=negb[F4, 0:1])
                                ACT(fsp[F4, :], e_[F4, :], AF.Ln, [ek], ["fsp"], bias=1.0)
                                TS("vector", fsp[F4, :], fsp[F4, :], 8.0, None, ALU.mult, None, ["fsp"], ["fsp"])
                                init = 0.0 if tb == 0 else P8[F4, tb * 512 - 1:tb * 512]
                                rd = ["fsp", "ones_f"] + ([("P8", tb - 1)] if tb else [])
                                S.op("vector", (lambda o_, d0, d1, i0: (lambda e: e.tensor_tensor_scan(
                                    out=o_, data0=d0, data1=d1, initial=i0, op0=ALU.mult, op1=ALU.add)))(
                                    P8[F4, cs], ones_f[F4, :], fsp[F4, :], init), rd, [("P8", tb)])
                            elif nm == "kr":
                                live["kr"] = (pt, pk)
                            elif nm == "krs":
                                pkr, pkrk = live["kr"]
                                a1, a1k = TMP()
                                a2, a2k = TMP()
                                TT("vector", a1[R, :], pkr[R, :], cosF[R, cs], ALU.mult, [pkrk, ("cosF", tb)], [a1k])
                                TT("vector", a2[R, :], pt[R, :], sinS[R, cs], ALU.mult, [pk, ("sinS", tb)], [a2k])
                                s1, s1k = STG()
                                TT("gpsimd", s1[R, :], a1[R, :], a2[R, :], ALU.add, [a1k, a2k], [s1k])
                                for hh in range(4):
                                    DMA("sync", xch_w[KTB + hh * 96 + 64:KTB + hh * 96 + 96, cs], s1[R, :], [s1k],
                                        [("xch_w", "kbr", hh, tb)])
                            elif nm in ("Cq0", "Cq1", "Ck0", "Ck1"):
                                j = int(nm[2])
                                s1, s1k = STG()
                                CP(ev, s1[:], pt[:], [pk], [s1k])
                                for hp in range(2):
                                    hh = 2 * j + hp
                                    if nm[1] == "q":
                                        DMA("sync", loc_w[QTC + hh * 70:QTC + hh * 70 + 64, cs], s1[hp * 64:(hp + 1) * 64, :],
                                            [s1k], [("loc_w", "qc", hh, tb)])
                                    else:
                                        DMA("sync", xch_w[KTC + hh * 73:KTC + hh * 73 + 64, cs], s1[hp * 64:(hp + 1) * 64, :],
                                            [s1k], [("xch_w", "kc", hh, tb)])
                            elif nm in ("val0", "val1"):
                                live["val"] = (pt, pk)
                            elif nm in ("gate0", "gate1"):
                                j = int(nm[4])
                                pv, pvk = live["val"]
                                sg, sgk = TMP()
                                ACT(sg[:], pt[:], AF.Sigmoid, [pk], [sgk])
                                u_, uk = TMP()
                                TT("vector", u_[:], pv[:], sg[:], ALU.mult, [pvk, sgk], [uk])
                                DMA("sync", uloc_w[j * 128:(j + 1) * 128, cs], u_[:], [uk], [("uloc_w", j, tb)])
                                if tb == NB - 1:
                                    s1, s1k = STG()
                                    CP("vector", s1[:, 0:32], u_[:, 480:512], [uk], [s1k])
                                    ut = xch_w[UT:UT + 4, :].rearrange("r (a c) -> (r a) c", c=32)
                                    DMA("sync", ut[j * 128:(j + 1) * 128, :], s1[:, 0:32], [s1k], [("xch_w", "ut", j)])
                F4 = slice(0, 4)
                negP = sbt(st, "negP", [128, 1], F32)
                TS("vector", negP[F4, :], P8[F4, T - 1:T], -1.0, None, ALU.mult, None, [("P8", NB - 1)], ["negP"])
                for tb in range(NB):
                    cs = slice(tb * 512, (tb + 1) * 512)
                    srcs = []
                    v0, v0k = TMP()
                    TS("vector", v0[F4, :], P8[F4, cs], -1.0, None, ALU.mult, None, [("P8", tb)], [v0k])
                    v2, v2k = TMP()
                    TS("vector", v2[F4, :], P8[F4, cs], negP[F4, 0:1], None, ALU.add, None, [("P8", tb), "negP"], [v2k])
                    for vi, (src, srck) in enumerate([(v0, v0k), (None, None), (v2, v2k)]):
                        if src is None:
                            w_, wk_ = TMP()
                            CP("vector", w_[F4, :], P8[F4, cs], [("P8", tb)], [wk_])
                            src, srck = w_, wk_
                        for part in range(3):
                            dst = fparts[F4, vi * 3 + part, :]
                            dk = ("fparts", vi * 3 + part)
                            CP("vector", dst, src[F4, :], [srck], [dk])
                            if part < 2:
                                TT("vector", src[F4, :], src[F4, :], dst, ALU.subtract, [srck, dk], [srck])
                    for hh in range(4):
                        for part in range(3):
                            DMA("sync", loc_w[QTC + hh * 70 + 64 + part:QTC + hh * 70 + 65 + part, cs],
                                fparts[hh:hh + 1, part, :], [("fparts", part)], [("loc_w", "qcf", hh, part, tb)])
                            DMA("sync", loc_w[QTC + hh * 70 + 67 + part:QTC + hh * 70 + 68 + part, cs],
                                ones_bf[0:1, :], ["ones_bf"], [("loc_w", "qco", hh, part, tb)])
                            DMA("sync", xch_w[KTC + hh * 73 + 64 + part:KTC + hh * 73 + 65 + part, cs],
                                ones_bf[0:1, :], ["ones_bf"], [("xch_w", "kco", hh, part, tb)])
                            DMA("sync", xch_w[KTC + hh * 73 + 67 + part:KTC + hh * 73 + 68 + part, cs],
                                fparts[hh:hh + 1, 3 + part, :], [("fparts", 3 + part)], [("xch_w", "kcf", hh, part, tb)])
                            DMA("sync", xch_w[KTC + hh * 73 + 70 + part:KTC + hh * 73 + 71 + part, cs],
                                fparts[hh:hh + 1, 6 + part, :], [("fparts", 6 + part)], [("xch_w", "kcg", hh, part, tb)])
                S.barrier()

        def phase23(l):
            li = lidx[l]
            xch_r, xprev_r, loc_r, uloc_r = bufs["xch_r"], bufs["xprev_r"], bufs["loc_r"], bufs["uloc_r"]
            S.barrier()
            with contextlib.ExitStack() as sbo:
                bo = [sbt(sbo, f"bo{n}", [128, 2, T], BF16) for n in range(4)]
                with contextlib.ExitStack() as st:
                    ub = sbt(st, "ub", [128, 2, 32 + T], F32)
                    acc = sbt(st, "acc", [128, 2, T], F32)
                    utb = sbt(st, "utb", [128, 2, 32], BF16)
                    cb = [sbt(st, f"cb{i}", [128, 512], BF16) for i in range(4)]
                    ct = [sbt(st, f"ct{i}", [128, 512], F32) for i in range(6)]
                    cctr = [0]

                    def CT():
                        i = cctr[0] % 6
                        cctr[0] += 1
                        return ct[i], ("ct", i)

                    DMA("sync", ub[:, :, 32:32 + T], uloc_r.rearrange("(c p) t -> p c t", p=128), (), ["ub"])
                    ut = xprev_r[UT:UT + 4, :].rearrange("r (a c) -> (r a) c", c=32)
                    DMA("sync", utb[:], ut.rearrange("(c p) k -> p c k", p=128), (), ["utb"])
                    TS("vector", ub[:, :, 0:32], utb[:], cst[:, 3:4], None, ALU.mult, None, ["utb", "cst"], ["ubh"])
                    for c in range(2):
                        w0 = vec[:, li, 58 + c * 31:59 + c * 31]
                        TS("vector", acc[:, c, :], ub[:, c, 2:2 + T], w0, vec[:, li, 51 + c:52 + c], ALU.mult, ALU.add,
                           ["ub", "ubh", "vec"], [("acc", c)])
                        for j in range(1, 31):
                            wj = vec[:, li, 58 + c * 31 + j:59 + c * 31 + j]
                            STT(acc[:, c, :], ub[:, c, 2 + j:2 + j + T], wj, acc[:, c, :], ALU.mult, ALU.add,
                                ["ub", "ubh", "vec", ("acc", c)], [("acc", c)])
                    for tb in range(NB):
                        cs = slice(tb * 512, (tb + 1) * 512)
                        pm, pmk = PS()
                        pq, pqk = PS()
                        for c in range(2):
                            CP("vector", cb[c][:], acc[:, c, cs], [("acc", c)], [("cb", c)])
                            ACT(cb[2 + c][:], acc[:, c, cs], AF.Square, [("acc", c)], [("cb", 2 + c)])
                        for c in range(2):
                            MM(pm[:], on256[:], cb[c][:], c == 0, c == 1, [("cb", c), "on256"], [pmk])
                        for c in range(2):
                            MM(pq[:], on256[:], cb[2 + c][:], c == 0, c == 1, [("cb", 2 + c), "on256"], [pqk])
                        m2, m2k = CT()
                        ACT(m2[:], pm[:], AF.Square, [pmk], [m2k])
                        var, vk = CT()
                        TT("vector", var[:], pq[:], m2[:], ALU.subtract, [pqk, m2k], [vk])
                        rs, rsk = CT()
                        tl, tlk = CT()
                        rstd_from(var[:], rs[:], tl[:], [vk], rsk, tlk)
                        for c in range(2):
                            d_, dk = CT()
                            TT("vector", d_[:], acc[:, c, cs], pm[:], ALU.subtract, [("acc", c), pmk], [dk])
                            TT("gpsimd", d_[:], d_[:], rs[:], ALU.mult, [dk, rsk], [dk])
                            TS("vector", d_[:], d_[:], vec[:, li, 53 + c:54 + c], vec[:, li, 55 + c:56 + c], ALU.mult, ALU.add,
                               [dk, "vec"], [dk])
                            ACT(bo[3][:, c, cs], d_[:], AF.Silu, [dk], [("bo", 3, c, tb)])
                S.barrier()
                with contextlib.ExitStack() as st:
                    kb = [sbt(st, f"kb{i}", [128, 2 * T], BF16) for i in range(2)]
                    qb = [sbt(st, f"qb{i}", [128, T], BF16) for i in range(2)]
                    vb_ = [sbt(st, f"vb{i}", [128, 32, 128], BF16) for i in range(2)]
                    pts = [sbt(st, f"pt{i}", [128, 512], BF16) for i in range(4)]
                    ntm = [sbt(st, f"ntm{i}", [128, 512], F32) for i in range(4)]
                    MSET("gpsimd", vb_[0][:, :, 64:128], 1.0, [("vbo", 0)])
                    MSET("gpsimd", vb_[1][:, :, 0:64], 1.0, [("vbo", 1)])
                    hc = [0]
                    sctr = [0]
                    nctr = [0]

                    def attn_mixer(mx, masks, biasA):
                        KR = {"A": 64, "B": 96, "C": 70}[mx]
                        scale = {"A": 0.125, "B": float(96 ** -0.5), "C": 0.125}[mx]
                        qbase, qstr = {"A": (QTA, 64), "B": (QTB, 96), "C": (QTC, 70)}[mx]
                        kbase, kstr = {"A": (KTA, 64), "B": (KTB, 96), "C": (KTC, 73)}[mx]
                        vbase = {"A": VA, "B": VB, "C": VC}[mx]
                        nbr = {"A": 0, "B": 1, "C": 2}[mx]
                        for hh in range(4):
                            slot = hc[0] % 2
                            hc[0] += 1
                            kt_, q_, v_ = kb[slot], qb[slot], vb_[slot]
                            kk, qk, vk = ("kb", slot), ("qb", slot), ("vb", slot)
                            voff = 0 if slot == 0 else 64
                            ooff = voff
                            soff = 64 - voff
                            r0 = kbase + hh * kstr
                            DMA("sync", kt_[0:KR, T:2 * T], xch_r[r0:r0 + KR, :], (), [kk])
                            if mx == "C":
                                DMA("sync", kt_[0:67, 0:T], xprev_r[r0:r0 + 67, :], (), [kk])
                                DMA("sync", kt_[67:70, 0:T], xprev_r[r0 + 70:r0 + 73, :], (), [kk])
                            else:
                                DMA("sync", kt_[0:KR, 0:T], xprev_r[r0:r0 + KR, :], (), [kk])
                            q0 = qbase + hh * qstr
                            DMA("sync", q_[0:KR, :], loc_r[q0:q0 + KR, :], (), [qk])
                            vp = xprev_r[vbase:vbase + 256, :].rearrange("r (a c) -> (r a) c", c=256).rearrange("(kt p) c -> p kt c", p=128)
                            vo = xch_r[vbase:vbase + 256, :].rearrange("r (a c) -> (r a) c", c=256).rearrange("(kt p) c -> p kt c", p=128)
                            DMA("sync", v_[:, 0:16, voff:voff + 64], vp[:, :, hh * 64:(hh + 1) * 64], (), [vk])
                            DMA("sync", v_[:, 16:32, voff:voff + 64], vo[:, :, hh * 64:(hh + 1) * 64], (), [vk])
                            for jb in range(NB):
                                if mx == "A":
                                    tiles = [(12 + 4 * jb + kt, ("prev" if 12 + 4 * jb + kt < 16 else "own"), 0, kt) for kt in range(8)]
                                else:
                                    tiles = [(g, "prev", 0, None) for g in range(16)]
                                    tiles += [(16 + g, "own", 0, None) for g in range(4 * jb)]
                                    tiles += [(16 + 4 * jb + j, "diag", j, None) for j in range(4)]
                                O, Ok = psum[4 + jb % 2], ("ps", 4 + jb % 2)
                                for i, (g, kind, j, kt) in enumerate(tiles):
                                    n0 = 128 * j
                                    si = sctr[0] % 4
                                    sctr[0] += 1
                                    Sp, Sk = psum[si], ("ps", si)
                                    pt_, ptk = pts[si], ("pt", si)
                                    need_mask = (kind == "diag") or (mx == "A")
                                    MM(Sp[:, n0:512], kt_[0:KR, g * 128:(g + 1) * 128], q_[0:KR, jb * 512 + n0:(jb + 1) * 512],
                                       True, not need_mask, [kk, qk], [Sk])
                                    if kind == "diag":
                                        MM(Sp[:, n0:512], idn[:], masks[:, j, n0:512], False, True, ["idn", "mbc"], [Sk])
                                    elif mx == "A":
                                        MM(Sp[:, n0:512], idn[:], biasA[:, hh * 8 + kt, :], False, True, ["idn", ("biasA", hh, kt)], [Sk])
                                    if kind == "prev":
                                        ACT(pt_[:, n0:512], Sp[:, n0:512], AF.Exp, [Sk, "cst"], [ptk], scale=scale, bias=cst[:, 2:3])
                                    else:
                                        ACT(pt_[:, n0:512], Sp[:, n0:512], AF.Exp, [Sk], [ptk], scale=scale)
                                    MM(O[:, n0:512], v_[:, g, :], pt_[:, n0:512], i == 0, i == len(tiles) - 1,
                                       [vk, ("vbo", slot), ptk], [Ok])
                                ni = nctr[0] % 2
                                nctr[0] += 1
                                t1, t1k = ntm[2 * ni], ("ntm", 2 * ni)
                                rc, rck = ntm[2 * ni + 1], ("ntm", 2 * ni + 1)
                                OS = slice(ooff, ooff + 64)
                                SS = slice(soff, soff + 64)
                                ACT(t1[OS, :], O[SS, :], AF.Ln, [Ok], [t1k])
                                ACT(rc[OS, :], t1[OS, :], AF.Exp, [t1k], [rck], scale=-1.0)
                                TT("vector", bo[nbr][OS, hh // 2, jb * 512:(jb + 1) * 512], O[OS, :], rc[OS, :], ALU.mult,
                                   [Ok, rck], [("bo", nbr, hh // 2, jb, hh % 2)])

                    with contextlib.ExitStack() as st2:
                        mbc = sbt(st2, "mbc", [128, 8, 512], BF16)
                        DMA("gpsimd", mbc[:], mbc_in.rearrange("k p n -> p k n"), (), ["mbc"])
                        attn_mixer("C", mbc[:, 4:8, :], None)
                        attn_mixer("B", mbc[:, 0:4, :], None)
                    S.barrier()
                    with contextlib.ExitStack() as st2:
                        biasA = sbt(st2, "biasA", [128, 32, 512], BF16)
                        mAs = sbt(st2, "mAs", [128, 8, 512], BF16)
                        rl = [sbt(st2, f"rl{i}", [128, 512], F32) for i in range(2)]
                        DMA("gpsimd", mAs[:], mA_in.rearrange("k p n -> p k n"), (), ["mAs"])
                        for hh in range(4):
                            for kt in range(8):
                                i = (hh * 8 + kt) % 2
                                DMA("sync", rl[i][:], relT_in[li, hh, kt], (), [("rl", i)])
                                STT(biasA[:, hh * 8 + kt, :], rl[i][:], 8.0, mAs[:, kt, :], ALU.mult, ALU.add,
                                    [("rl", i), "mAs"], [("biasA", hh, kt)])
                        attn_mixer("A", None, biasA)
                S.barrier()
                with contextlib.ExitStack() as st:
                    h3 = sbt(st, "h3", [128, 8, T], BF16)
                    m = sbt(st, "m", [128, 8, T], BF16)
                    with contextlib.ExitStack() as st2:
                        rmsnorm_to(h3, "h3", 0, li, st2)
                    S.barrier()
                    wgb = [sbt(st, f"wgb{i}", [128, 8, 512], BF16) for i in range(2)]
                    wbb = [sbt(st, f"wbb{i}", [128, 8, 128], BF16) for i in range(2)]
                    mt = [sbt(st, f"mt{i}", [128, 512], F32) for i in range(6)]
                    mctr = [0]

                    def MT():
                        i = mctr[0] % 6
                        mctr[0] += 1
                        return mt[i], ("mt", i)

                    wgv = W["wg"][li].rearrange("(kc p) (n d) -> p kc n d", p=128, n=4)
                    wbv = W["wbr"][li].rearrange("n (c p) d -> p n c d", p=128)

                    def load_m(dt):
                        for n in range(4):
                            DMA("gpsimd", wgb[dt % 2][:, :, n * 128:(n + 1) * 128], wgv[:, :, n, dt * 128:(dt + 1) * 128], (), [("wgb", dt % 2)])
                            DMA("gpsimd", wbb[dt % 2][:, 2 * n:2 * n + 2, :], wbv[:, n, :, dt * 128:(dt + 1) * 128], (), [("wbb", dt % 2)])

                    load_m(0)
                    for dt in range(8):
                        if dt + 1 < 8:
                            load_m(dt + 1)
                        wg_, wb_ = wgb[dt % 2], wbb[dt % 2]
                        for tb in range(NB):
                            cs = slice(tb * 512, (tb + 1) * 512)
                            macc, mk = MT()
                            for n in range(4):
                                pg, pgk = PS()
                                for c in range(8):
                                    MM(pg[:], wg_[:, c, n * 128:(n + 1) * 128], h3[:, c, cs], c == 0, c == 7,
                                       [("wgb", dt % 2), ("h3", c, tb)], [pgk])
                                pp, ppk = PS()
                                for c in range(2):
                                    MM(pp[:], wb_[:, 2 * n + c, :], bo[n][:, c, cs], c == 0, c == 1,
                                       [("wbb", dt % 2), ("bo", n, c, tb), ("bo", n, c, tb, 0), ("bo", n, c, tb, 1)], [ppk])
                                gt, gk = MT()
                                ACT(gt[:], pg[:], AF.Sigmoid, [pgk, "vec"], [gk], bias=vec[:, li, 19 + n * 8 + dt:20 + n * 8 + dt])
                                if n == 0:
                                    TT("vector", macc[:], pp[:], gt[:], ALU.mult, [ppk, gk], [mk])
                                else:
                                    TT("vector", gt[:], pp[:], gt[:], ALU.mult, [ppk, gk], [gk])
                                    if n < 3:
                                        TT("gpsimd", macc[:], macc[:], gt[:], ALU.add, [mk, gk], [mk])
                                    else:
                                        TT("gpsimd", m[:, dt, cs], macc[:], gt[:], ALU.add, [mk, gk], [("m", dt, tb)])
                    wov = W["wo"][li].rearrange("(kc p) n -> p kc n", p=128)
                    for g in range(2):
                        DMA("gpsimd", wgb[g][:], wov[:, :, g * 512:(g + 1) * 512], (), [("wgb", g)])
                    for do in range(8):
                        wo_ = wgb[do // 4]
                        for tb in range(NB):
                            cs = slice(tb * 512, (tb + 1) * 512)
                            po, pok = PS()
                            for c in range(8):
                                MM(po[:], wo_[:, c, (do % 4) * 128:(do % 4 + 1) * 128], m[:, c, cs], c == 0, c == 7,
                                   [("wgb", do // 4), ("m", c, tb)], [pok])
                            TT("vector", xT[:, do, cs], po[:], xT[:, do, cs], ALU.add, [pok, ("xT", do, tb)], [("xT", do, tb)])
            S.barrier()
            with contextlib.ExitStack() as st:
                hf = sbt(st, "hf", [128, 8, T], BF16)
                hid = [sbt(st, f"hid{i}", [128, 4, T], BF16) for i in range(2)]
                wub = [sbt(st, f"wub{i}", [128, 8, 512], BF16) for i in range(2)]
                wdb = [sbt(st, f"wdb{i}", [128, 4, 1024], BF16) for i in range(2)]
                rt = [sbt(st, f"rt{i}", [128, 512], F32) for i in range(4)]
                rctr = [0]
                with contextlib.ExitStack() as st2:
                    rmsnorm_to(hf, "hf", 8, li, st2)
                wuv = W["wup"][li].rearrange("(kc p) n -> p kc n", p=128)
                wdv = W["wdn"][li].rearrange("(fc p) n -> p fc n", p=128)

                def load_f(G):
                    DMA("gpsimd", wub[G % 2][:], wuv[:, :, G * 512:(G + 1) * 512], (), [("wub", G % 2)])
                    DMA("gpsimd", wdb[G % 2][:], wdv[:, G * 4:(G + 1) * 4, :], (), [("wdb", G % 2)])

                def up(G):
                    for f in range(4):
                        for tb in range(NB):
                            cs = slice(tb * 512, (tb + 1) * 512)
                            pu, puk = PS()
                            for c in range(8):
                                MM(pu[:], wub[G % 2][:, c, f * 128:(f + 1) * 128], hf[:, c, cs], c == 0, c == 7,
                                   [("wub", G % 2), ("hf", c, tb)], [puk])
                            i = rctr[0] % 4
                            rctr[0] += 1
                            ACT(rt[i][:], pu[:], AF.Relu, [puk], [("rt", i)])
                            TT("gpsimd", hid[G % 2][:, f, cs], rt[i][:], rt[i][:], ALU.mult, [("rt", i)], [("hid", G % 2, f, tb)])

                def down(G):
                    for do in range(8):
                        for tb in range(NB):
                            cs = slice(tb * 512, (tb + 1) * 512)
                            pd, pdk = PS()
                            for f in range(4):
                                MM(pd[:], wdb[G % 2][:, f, do * 128:(do + 1) * 128], hid[G % 2][:, f, cs], f == 0, f == 3,
                                   [("wdb", G % 2), ("hid", G % 2, f, tb)], [pdk])
                            TT("vector", xT[:, do, cs], pd[:], xT[:, do, cs], ALU.add, [pdk, ("xT", do, tb)], [("xT", do, tb)])

                load_f(0)
                up(0)
                for G in range(8):
                    if G + 1 < 8:
                        load_f(G + 1)
                        up(G + 1)
                    down(G)
            S.barrier()

        def phase_final():
            S.barrier()
            with contextlib.ExitStack() as st:
                sq = [sbt(st, f"fsq{i}", [128, 512], BF16) for i in range(3)]
                rs = [sbt(st, f"frs{i}", [128, 512], F32) for i in range(2)]
                tl = [sbt(st, f"ftl{i}", [128, 512], F32) for i in range(2)]
                yo = [sbt(st, f"fyo{i}", [128, 512], F32) for i in range(4)]
                k = 0
                y = 0
                yv = yT_out.rearrange("(c p) t -> p c t", p=128)
                for tb in range(NB):
                    cs = slice(tb * 512, (tb + 1) * 512)
                    pt, pk = PS()
                    for c in range(8):
                        s_, sk = sq[k % 3], ("fsq", k % 3)
                        k += 1
                        ACT(s_[:], xT[:, c, cs], AF.Square, [("xT", c, tb)], [sk])
                        MM(pt[:], on1024[:], s_[:], c == 0, c == 7, [sk, "on1024"], [pk])
                    rstd_from(pt[:], rs[tb % 2][:], tl[tb % 2][:], [pk], ("frs", tb % 2), ("ftl", tb % 2))
                    for c in range(8):
                        yo_, yk = yo[y % 4], ("fyo", y % 4)
                        y += 1
                        STT(yo_[:], xT[:, c, cs], gfin[:, c:c + 1], rs[tb % 2][:], ALU.mult, ALU.mult,
                            [("xT", c, tb), "gfin", ("frs", tb % 2)], [yk])
                        DMA("sync", yv[:, c, cs], yo_[:], [yk], [("yT", c, tb)])
            S.barrier()

        for ph in phases:
            if ph.startswith("P1"):
                phase1(int(ph.split("_")[1]))
            elif ph.startswith("P23"):
                phase23(int(ph.split("_")[1]))
            elif ph == "FIN":
                phase_final()

        S.barrier()
        DMA("sync", xT_out.rearrange("(c p) t -> p c t", p=128), xT[:], XK, ["xT_out"])
        S.op("sync", None, ["xT_out"], ())
        S.barrier()
        S.emit()
    S.names_in = names_in
    return nc, S


def _consts():
    idn = np.eye(128, dtype=np.float32)
    s_ = np.arange(128)[:, None]
    t_ = np.arange(512)[None, :]
    mbc = np.zeros((8, 128, 512), np.float32)
    for j in range(4):
        sk = 128 * j + s_
        mbc[j] = np.where((sk // 64) <= (t_ // 64), 0.0, NEG)
        mbc[4 + j] = np.where(sk <= t_, 0.0, NEG)
    mA = np.zeros((8, 128, 512), np.float32)
    for kt in range(8):
        dch = (512 + t_) // 64 - (128 * kt + s_) // 64
        mA[kt] = np.where((dch >= 0) & (dch <= 8), 0.0, 8 * NEG)
    half = 16
    inv = (1.0 / (np.float32(10000.0) ** (np.arange(half, dtype=np.float32) / np.float32(half)))).astype(np.float32)
    return idn, mbc, mA, inv


def _layout_weights(inp, layers):
    ls = list(layers)
    w_in = inp["w_in"]
    wi = np.zeros((len(ls), D, WI_COLS), np.float32)

    def put(dst0, src0, n):
        wi[:, :, dst0:dst0 + n] = w_in[ls][:, :, src0:src0 + n]

    put(0, 0, 256)
    put(256, 256, 256)
    put(512, 768, 256)
    put(768, 1024, 128)
    put(896, 1952, 4)
    put(1024 + 64, 1152, 32)
    put(1152 + 64, 1152 + 16, 16)
    put(1152 + 80, 1152, 16)
    put(1280, 1184, 256)
    put(1536, 1440, 256)
    put(1792, 1956, 128)
    put(1920, 2212, 128)
    put(2048, 1956 + 128, 128)
    put(2176, 2212 + 128, 128)
    put(2560, 512, 256)
    put(2816, 1696, 256)
    wuq = np.zeros((len(ls), 256, 768), np.float32)
    wq = inp["w_uq"][ls]
    wuq[:, :, 0:384] = wq
    for h in range(4):
        wuq[:, :, 384 + h * 96 + 64:384 + h * 96 + 80] = wq[:, :, h * 96 + 80:h * 96 + 96]
        wuq[:, :, 384 + h * 96 + 80:384 + h * 96 + 96] = wq[:, :, h * 96 + 64:h * 96 + 80]
    wk = inp["w_ukv"][ls]
    wukv = np.zeros((len(ls), 128, 512), np.float32)
    for h in range(4):
        wukv[:, :, h * 64:(h + 1) * 64] = wk[:, :, h * 128:h * 128 + 64]
        wukv[:, :, 256 + h * 64:256 + (h + 1) * 64] = wk[:, :, h * 128 + 64:h * 128 + 128]
    vec = np.zeros((len(ls), 128, NVEC), np.float32)

    def cols(a, n):
        return a.reshape(len(ls), n, 128).transpose(0, 2, 1)

    vec[:, :, 0:8] = cols(inp["g_mix"][ls], 8)
    vec[:, :, 8:16] = cols(inp["g_ffn"][ls], 8)
    vec[:, :, 16:18] = cols(inp["g_q_lat"][ls], 2)
    vec[:, :, 18:19] = cols(inp["g_kv_lat"][ls], 1)
    vec[:, :, 19:51] = cols(inp["b_gate"][ls], 32)
    vec[:, :, 51:53] = cols(inp["b_dw"][ls], 2)
    vec[:, :, 53:55] = cols(inp["g_conv_ln"][ls], 2)
    vec[:, :, 55:57] = cols(inp["b_conv_ln"][ls], 2)
    vec[:, 0:4, 57] = inp["b_forget"][ls]
    wdw = inp["w_dw"][ls]
    for c in range(2):
        vec[:, :, 58 + c * 31:58 + (c + 1) * 31] = wdw[:, :, c * 128:(c + 1) * 128].transpose(0, 2, 1)
    s_ = np.arange(128)[:, None]
    t_ = np.arange(512)[None, :]
    relT = np.zeros((len(ls), 4, 8, 128, 512), np.float32)
    rb = inp["rel_bias"][ls]
    for kt in range(8):
        idx = np.clip(t_ - s_ + 512 - 128 * kt, -128, 128) + 128
        relT[:, :, kt] = rb[:, :, idx]
    out = dict(wi=wi, wuq=wuq, wukv=wukv, vec_in=vec, relT=relT,
               wg=np.ascontiguousarray(inp["w_gate"][ls]), wbr=np.ascontiguousarray(inp["w_branch"][ls]),
               wo=np.ascontiguousarray(inp["w_o"][ls]), wup=np.ascontiguousarray(inp["w_up"][ls]),
               wdn=np.ascontiguousarray(inp["w_down"][ls]))
    return out


_PROGS = {}


def _prog(phases):
    key = tuple(phases)
    if key not in _PROGS:
        _PROGS[key] = build(list(phases), False)
    return _PROGS[key]


def _launch(phases, maps):
    nc, S = _prog(phases)
    maps = [{k: m[k] for k in S.names_in} for m in maps]
    return run_bass_kernel_spmd(nc, maps, core_ids=list(range(len(maps)))).results


def _core_common(inp, c):
    b, half = c // 2, c % 2
    idn, mbc, mA, inv = _consts()
    cst = np.zeros((128, 8), np.float32)
    cst[64:80, 0] = inv
    cst[80:96, 0] = inv
    cst[64:80, 1] = -1.0
    cst[80:96, 1] = 1.0
    cst[:, 2] = NEG if half == 0 else 0.0
    cst[:, 3] = 0.0 if half == 0 else 1.0
    pos = np.ascontiguousarray(np.broadcast_to(inp["positions"][b, half * T:(half + 1) * T][None, :], (32, T))).astype(np.int32)
    gfin = np.ascontiguousarray(inp["g_final"].reshape(8, 128).T)
    return dict(pos_in=pos, cst_in=cst, gfin_in=gfin, idn_in=idn, mbc_in=mbc, mA_in=mA)


def kernel(**inp):
    inp = {k: np.asarray(v) for k, v in inp.items()}
    x = inp["x"]
    ncores = 8
    common = [_core_common(inp, c) for c in range(ncores)]
    xT = [np.ascontiguousarray(x[c // 2, (c % 2) * T:(c % 2 + 1) * T, :].T) for c in range(ncores)]
    w0 = _layout_weights(inp, [0])
    w1 = _layout_weights(inp, [1])
    w01 = {k: np.concatenate([w0[k], w1[k]], 0) for k in w0}
    maps = [dict(common[c], xT_in=xT[c], **w0) for c in range(ncores)]
    r1 = _launch(["P1_0"], maps)
    maps = []
    for c in range(ncores):
        p = c - 1 if c % 2 == 1 else c
        maps.append(dict(common[c], xT_in=r1[c]["xT_out"], xch_i=r1[c]["xch_o"], xprev_i=r1[p]["xch_o"],
                         loc_i=r1[c]["loc_o"], uloc_i=r1[c]["uloc_o"], **w01))
    r2 = _launch(["P23_0", "P1_1"], maps)
    maps = []
    for c in range(ncores):
        p = c - 1 if c % 2 == 1 else c
        maps.append(dict(common[c], xT_in=r2[c]["xT_out"], xch_i=r2[c]["xch_o"], xprev_i=r2[p]["xch_o"],
                         loc_i=r2[c]["loc_o"], uloc_i=r2[c]["uloc_o"], **w1))
    r3 = _launch(["P23_1", "FIN"], maps)
    out = np.zeros((4, 4096, D), np.float32)
    for c in range(ncores):
        out[c // 2, (c % 2) * T:(c % 2 + 1) * T, :] = r3[c]["yT_out"].T
    return out
```
